# Optimizing a Trainium2 kernel written in Bass

```python
import jax, jax.numpy as jnp
from jax import lax
import numpy as np

D_MODEL = 1024
BATCH = 8
SEQ = 2048
DEPTH = 2
DEC_BATCH = 128
DEC_SEQ = 4
PAST_LEN = 16384
PAGE_SIZE = 128

WIDTH_A = D_MODEL
WIDTH_B = D_MODEL
LRU_HEADS = 16
LRU_BLOCK = WIDTH_B // LRU_HEADS
CONV_A_WIDTH = 3
CONV_B_WIDTH = 4
LRU_C = 8.0
PLE_DIM = 256
RMS_EPS = 1e-6
SPLIT_SIZES = [WIDTH_A, WIDTH_A, WIDTH_A, WIDTH_A, WIDTH_B, WIDTH_B, D_MODEL, D_MODEL]
IN_COLS = sum(SPLIT_SIZES)
SPLIT_POINTS = [int(v) for v in np.cumsum(SPLIT_SIZES)[:-1]]

kernel_name = "hybrid_gconv_rglru_parallel_step"


def _rmsnorm(x, g):
    x32 = x.astype(jnp.float32)
    y = x32 * lax.rsqrt(jnp.mean(x32 * x32, axis=-1, keepdims=True) + RMS_EPS)
    return (y * g.astype(jnp.float32)).astype(x.dtype)


def _causal_dwconv(x, buf, w):
    k_width = w.shape[0]
    t_len = x.shape[1]
    xp = jnp.concatenate([buf.astype(x.dtype), x], axis=1)
    y = xp[:, 0:t_len] * w[0]
    for k in range(1, k_width):
        y = y + xp[:, k:k + t_len] * w[k]
    return y, xp[:, t_len:]


def _rg_lru(xc, h0, w_r, b_r, w_i, b_i, lam):
    bsz, t_len, width = xc.shape
    x32 = xc.astype(jnp.float32)
    xh = x32.reshape(bsz, t_len, LRU_HEADS, LRU_BLOCK)
    r = jax.nn.sigmoid(jnp.einsum('bthi,hij->bthj', xh, w_r.astype(jnp.float32)).reshape(bsz, t_len, width) + b_r.astype(jnp.float32))
    gi = jax.nn.sigmoid(jnp.einsum('bthi,hij->bthj', xh, w_i.astype(jnp.float32)).reshape(bsz, t_len, width) + b_i.astype(jnp.float32))
    log_a = -LRU_C * r * jax.nn.softplus(-lam.astype(jnp.float32))
    a = jnp.exp(log_a)
    beta = jnp.sqrt(-jnp.expm1(2.0 * log_a))
    u = beta * gi * x32

    def step(h, inp):
        a_t, u_t = inp
        h = a_t * h + u_t
        return h, h

    h_last, hs = lax.scan(step, h0.astype(jnp.float32), (jnp.swapaxes(a, 0, 1), jnp.swapaxes(u, 0, 1)))
    return jnp.swapaxes(hs, 0, 1).astype(xc.dtype), h_last


def _layer(x, p, buf_a, buf_b, h0, norm_in, w_in, conv_a_w, conv_b_w, conv_b_b,
           w_r, b_r, w_i, b_i, lam, w_a_out, w_b_out, w_o, norm_pe, w_pg, w_pe):
    hn = _rmsnorm(x, norm_in)
    z = hn @ w_in
    u_a, b_a, c_a, g_a, x_b, g_b, m_a, m_b = jnp.split(z, SPLIT_POINTS, axis=-1)
    conv_out, new_a = _causal_dwconv(c_a * u_a, buf_a, conv_a_w)
    y_a = b_a * conv_out * jax.nn.silu(g_a)
    xc, new_b = _causal_dwconv(x_b, buf_b, conv_b_w)
    xc = xc + conv_b_b
    hs, h_last = _rg_lru(xc, h0, w_r, b_r, w_i, b_i, lam)
    y_b = hs * jax.nn.silu(g_b)
    merged = jax.nn.sigmoid(m_a) * (y_a @ w_a_out) + jax.nn.sigmoid(m_b) * (y_b @ w_b_out)
    x = x + merged @ w_o
    gate = jax.nn.sigmoid(_rmsnorm(x, norm_pe) @ w_pg)
    x = x + (p @ w_pe) * gate
    return x, new_a, new_b, h_last


def setup_inputs(seed: int = 0) -> dict:
    key = jax.random.key(seed)
    ks = jax.random.split(key, 24)
    nrm = jax.random.normal
    f32 = jnp.float32
    a0 = jax.random.uniform(ks[18], (DEPTH, WIDTH_B), f32, minval=0.9, maxval=0.999)
    return {
        "x_prompt": nrm(ks[0], (BATCH, SEQ, D_MODEL), f32),
        "x_sample": nrm(ks[1], (DEC_BATCH, DEC_SEQ, D_MODEL), f32),
        "state_conv_a": nrm(ks[2], (DEPTH, DEC_BATCH, CONV_A_WIDTH - 1, WIDTH_A), f32),
        "state_conv_b": nrm(ks[3], (DEPTH, DEC_BATCH, CONV_B_WIDTH - 1, WIDTH_B), f32),
        "state_h": 0.5 * nrm(ks[4], (DEPTH, DEC_BATCH, WIDTH_B), f32),
        "p_prompt": nrm(ks[5], (DEPTH, BATCH, SEQ, PLE_DIM), f32),
        "p_sample": nrm(ks[6], (DEPTH, DEC_BATCH, DEC_SEQ, PLE_DIM), f32),
        "norm_in": 1.0 + 0.05 * nrm(ks[7], (DEPTH, D_MODEL), f32),
        "w_in": nrm(ks[8], (DEPTH, D_MODEL, IN_COLS), f32) * D_MODEL ** -0.5,
        "conv_a_w": nrm(ks[9], (DEPTH, CONV_A_WIDTH, WIDTH_A), f32) * CONV_A_WIDTH ** -0.5,
        "conv_b_w": nrm(ks[10], (DEPTH, CONV_B_WIDTH, WIDTH_B), f32) * CONV_B_WIDTH ** -0.5,
        "conv_b_b": 0.02 * nrm(ks[11], (DEPTH, WIDTH_B), f32),
        "w_r": nrm(ks[12], (DEPTH, LRU_HEADS, LRU_BLOCK, LRU_BLOCK), f32) * LRU_BLOCK ** -0.5,
        "b_r": 0.02 * nrm(ks[13], (DEPTH, WIDTH_B), f32),
        "w_i": nrm(ks[14], (DEPTH, LRU_HEADS, LRU_BLOCK, LRU_BLOCK), f32) * LRU_BLOCK ** -0.5,
        "b_i": 0.02 * nrm(ks[15], (DEPTH, WIDTH_B), f32),
        "lam": jnp.log(a0) - jnp.log1p(-a0),
        "w_a_out": nrm(ks[16], (DEPTH, WIDTH_A, D_MODEL), f32) * WIDTH_A ** -0.5,
        "w_b_out": nrm(ks[17], (DEPTH, WIDTH_B, D_MODEL), f32) * WIDTH_B ** -0.5,
        "w_o": nrm(ks[19], (DEPTH, D_MODEL, D_MODEL), f32) * D_MODEL ** -0.5,
        "norm_pe": 1.0 + 0.05 * nrm(ks[20], (DEPTH, D_MODEL), f32),
        "w_pg": nrm(ks[21], (DEPTH, D_MODEL, D_MODEL), f32) * D_MODEL ** -0.5,
        "w_pe": nrm(ks[22], (DEPTH, PLE_DIM, D_MODEL), f32) * PLE_DIM ** -0.5,
        "norm_final": 1.0 + 0.05 * nrm(ks[23], (D_MODEL,), f32),
    }


def reference(x_prompt, x_sample, state_conv_a, state_conv_b, state_h, p_prompt, p_sample,
              norm_in, w_in, conv_a_w, conv_b_w, conv_b_b, w_r, b_r, w_i, b_i, lam,
              w_a_out, w_b_out, w_o, norm_pe, w_pg, w_pe, norm_final):
    xp, xs = x_prompt, x_sample
    n_p = x_prompt.shape[0]
    ca_p, cb_p, h_p, ca_s, cb_s, h_s = [], [], [], [], [], []
    for l in range(DEPTH):
        weights = (norm_in[l], w_in[l], conv_a_w[l], conv_b_w[l], conv_b_b[l], w_r[l], b_r[l],
                   w_i[l], b_i[l], lam[l], w_a_out[l], w_b_out[l], w_o[l], norm_pe[l], w_pg[l], w_pe[l])
        buf_a0 = jnp.zeros((n_p, CONV_A_WIDTH - 1, WIDTH_A), xp.dtype)
        buf_b0 = jnp.zeros((n_p, CONV_B_WIDTH - 1, WIDTH_B), xp.dtype)
        h00 = jnp.zeros((n_p, WIDTH_B), jnp.float32)
        xp, na, nb, nh = _layer(xp, p_prompt[l], buf_a0, buf_b0, h00, *weights)
        ca_p.append(na); cb_p.append(nb); h_p.append(nh)
        xs, na, nb, nh = _layer(xs, p_sample[l], state_conv_a[l], state_conv_b[l], state_h[l], *weights)
        ca_s.append(na); cb_s.append(nb); h_s.append(nh)
    y_prompt = _rmsnorm(xp, norm_final)
    y_sample = _rmsnorm(xs, norm_final)
    return (y_prompt, y_sample,
            jnp.stack(ca_p), jnp.stack(cb_p), jnp.stack(h_p),
            jnp.stack(ca_s), jnp.stack(cb_s), jnp.stack(h_s))
```

```python
import numpy as np
import concourse.bass as bass
import concourse.mybir as mybir
from concourse.bass_utils import run_bass_kernel_spmd

F32 = mybir.dt.float32
BF16 = mybir.dt.bfloat16
AF = mybir.ActivationFunctionType
ALU = mybir.AluOpType

D = 1024
KC = 8
L = 2
NCORES = 8
SEQ = 2048
SB = 16
ST = 4
NS = SB * ST
NTOK = SEQ + NS
NG = 1024 + NS
PSPLIT = 960
PLE = 256
INC = 8192
EPS = 1e-6
NSLOT = 5
SLOT_ELEMS = 4 * 1024

V_NIN = 0
V_CAW = 16
V_CBW = 64
V_CBB = 128
V_BR = 144
V_BI = 160
V_LAM = 176
V_NPE = 192
V_NF = 208
NV = 216
DV_CAWH = 0
DV_BRH = 48
DV_BIH = 64
DV_CH = 80
DV_CQ = 96
DV_EPS = 112
DV_TMP = 113
DV_C4 = 164
NDV = 200


class Prog:
    ENG = ("pe", "act", "dve", "pool", "sp")

    def __init__(self):
        self.stream = {e: [] for e in self.ENG}
        self.cnt = {}
        self.waited = {e: {} for e in self.ENG}
        self.lw = {}
        self.rd = {}

    def _deps(self, eng, reads, writes):
        need = {}

        def add(sv):
            s, v = sv
            if s == "pe" and eng == "pe":
                return
            if need.get(s, 0) < v:
                need[s] = v

        for k in reads:
            w = self.lw.get(k)
            if w:
                add(w)
        for k in writes:
            w = self.lw.get(k)
            if w:
                add(w)
            for r in self.rd.get(k, ()):
                add(r)
        for s, v in need.items():
            if self.waited[eng].get(s, 0) < v:
                self.waited[eng][s] = v
                self.stream[eng].append(("wait", s, v))

    def _record(self, sv, reads, writes):
        for k in writes:
            self.lw[k] = sv
            self.rd[k] = []
        for k in reads:
            self.rd.setdefault(k, []).append(sv)

    def op(self, eng, fn, reads=(), writes=()):
        if eng != "pe":
            psr = [k for k in reads if k[0] == "ps"]
            if psr:
                writes = list(writes) + psr
        self._deps(eng, reads, writes)
        v = self.cnt.get(eng, 0) + 1
        self.cnt[eng] = v
        self.stream[eng].append(("op", fn, eng, 1))
        self._record((eng, v), reads, writes)

    def dma(self, eng, fns, dsem, reads=(), writes=(), nodeps=False, extra=()):
        if not nodeps:
            self._deps(eng, reads, writes)
        for s_, v_ in extra:
            if self.waited[eng].get(s_, 0) < v_:
                self.waited[eng][s_] = v_
                self.stream[eng].append(("wait", s_, v_))
        for fn in fns:
            self.cnt[dsem] = self.cnt.get(dsem, 0) + 16
            self.stream[eng].append(("op", fn, dsem, 16))
        self._record((dsem, self.cnt[dsem]), reads, writes)

    def final_wait(self, eng, sems):
        for s in sems:
            v = self.cnt.get(s, 0)
            if v and self.waited[eng].get(s, 0) < v:
                self.waited[eng][s] = v
                self.stream[eng].append(("wait", s, v))


def build_nc():
    nc = bass.Bass("TRN2", target_bir_lowering=False)

    def din(name, shape):
        return nc.dram_tensor(name, shape, F32, kind="ExternalInput").ap()

    def dout(name, shape):
        return nc.dram_tensor(name, shape, F32, kind="ExternalOutput").ap()

    xin = din("xin", [128, KC, NTOK])
    pin = din("pin", [L, 128, 2, NTOK])
    scaT = din("scaT", [128, L, KC, SB, 2])
    scbT = din("scbT", [128, L, KC, SB, 3])
    shT = din("shT", [128, L, KC, SB])
    vecs_d = din("vecs", [128, NV])
    w_in = din("w_in", [L, D, INC])
    w_r = din("w_r", [L, 16, 64, 64])
    w_i = din("w_i", [L, 16, 64, 64])
    w_a_out = din("w_a_out", [L, D, D])
    w_b_out = din("w_b_out", [L, D, D])
    w_o = din("w_o", [L, D, D])
    w_pg = din("w_pg", [L, D, D])
    w_pe = din("w_pe", [L, PLE, D])
    yT = dout("yT", [128, KC, NTOK])
    sop_d = dout("sop", [128, L, KC, 6])
    sos_d = dout("sos", [128, L, KC, SB, 6])

    P = Prog()
    from contextlib import ExitStack
    with ExitStack() as ctx:
        def sb(name, shape, dt):
            return ctx.enter_context(nc.sbuf_tensor(name, shape, dt))

        x = sb("x", [128, KC, NG], F32)
        hn = sb("hn", [128, KC, NG], BF16)
        ya = sb("ya", [128, KC, NG], BF16)
        yb = sb("yb", [128, KC, NG], BF16)
        mg = sb("mg", [128, KC, NG], BF16)
        pT = sb("pT", [128, 2, NG], BF16)
        wring = [sb(f"wslot{i}", [128, SLOT_ELEMS], BF16) for i in range(NSLOT)]
        wrbd = sb("wrbd", [128, L, KC, 128], BF16)
        wibd = sb("wibd", [128, L, KC, 128], BF16)
        vecs = sb("vecs_sb", [128, NV], F32)
        dv = sb("dv", [128, NDV], F32)
        sA = sb("sA", [128, L, KC, SB, 2], F32)
        sBs = sb("sB", [128, L, KC, SB, 3], F32)
        sH = sb("sH", [128, L, KC, SB], F32)
        sop = sb("sop_sb", [128, L, KC, 6], F32)
        sos = sb("sos_sb", [128, L, KC, SB, 6], F32)
        ones = sb("ones", [128, 128], BF16)

        WK = {}
        wk_idx = {}

        def mkwork(name, cols, dt, nbuf):
            WK[name] = [sb(f"wk_{name}{i}", [128, cols], dt) for i in range(nbuf)]
            wk_idx[name] = 0

        def wk(name):
            i = wk_idx[name]
            wk_idx[name] = (i + 1) % len(WK[name])
            return WK[name][i], ("wk", name, i)

        def wk_peek(name):
            i = wk_idx[name]
            return WK[name][i], ("wk", name, i)

        mkwork("sq", 512, BF16, 3)
        mkwork("cu", 2 + 512, F32, 2)
        mkwork("cuS", SB * 6, F32, 1)
        mkwork("cv", 512, F32, 2)
        mkwork("tg", 512, F32, 2)
        mkwork("hb", 4, F32, 2)
        mkwork("xbS", SB * 7, F32, 2)
        mkwork("xc", 512, F32, 2)
        mkwork("xcb", 512, BF16, 2)
        mkwork("tr", 512, F32, 1)
        mkwork("ti", 512, F32, 1)
        mkwork("a", 512, F32, 1)
        mkwork("th", 512, F32, 1)
        mkwork("h", 512, F32, 2)
        mkwork("s2", 512, F32, 2)

        psum = [ctx.enter_context(nc.psum_tensor(f"ps{i}", [128, 512], F32)) for i in range(8)]
        ps_next = [0]

        def bank():
            b = ps_next[0]
            ps_next[0] = (b + 1) % 8
            return b

        sem_names = list(Prog.ENG) + [f"d_w{i}" for i in range(NSLOT)] + [f"d_x{i}" for i in range(3)] + [f"d_y{i}" for i in range(3)] + ["d_p", "d_misc", "d_misc2", "d_so"]
        SEM = {n: ctx.enter_context(nc.semaphore("sem_" + n)) for n in sem_names}

        def vcol(c):
            return vecs[:, c:c + 1]

        def dcol(c):
            return dv[:, c:c + 1]

        def act(out, in_, func, reads, writes, bias=None, scale=None):
            kw = {}
            if bias is not None:
                kw["bias"] = bias
            if scale is not None:
                kw["scale"] = scale
            P.op("act", lambda e: e.activation(out=out, in_=in_, func=func, **kw), reads, writes)

        def tt(out, in0, in1, op, reads, writes, eng="dve"):
            P.op(eng, lambda e: e.tensor_tensor(out=out, in0=in0, in1=in1, op=op), reads, writes)

        def ts(out, in0, s1, s2, op0, op1, reads, writes, eng="dve"):
            if s2 is None:
                P.op(eng, lambda e: e.tensor_scalar(out=out, in0=in0, scalar1=s1, scalar2=None, op0=op0),
                     reads, writes)
            else:
                P.op(eng, lambda e: e.tensor_scalar(out=out, in0=in0, scalar1=s1, scalar2=s2, op0=op0, op1=op1),
                     reads, writes)

        def stt(out, in0, scalar, in1, op0, op1, reads, writes):
            P.op("dve", lambda e: e.scalar_tensor_tensor(out=out, in0=in0, scalar=scalar, in1=in1, op0=op0, op1=op1),
                 reads, writes)

        def cp(out, in_, reads, writes, eng="dve"):
            P.op(eng, lambda e: e.tensor_copy(out=out, in_=in_), reads, writes)

        def mm(bk, n, pairs, reads):
            def fn(e):
                ins = None
                for i, (lh, rh) in enumerate(pairs):
                    ins = e.matmul(out=psum[bk][:, 0:n], lhsT=lh, rhs=rh, start=(i == 0), stop=(i == len(pairs) - 1))
                return ins
            P.op("pe", fn, reads, [("ps", bk)])

        def v3(ap, t):
            return ap.rearrange("p (s t) -> p s t", t=t)

        P.dma("sp", [lambda e: e.dma_start(out=vecs[:], in_=vecs_d),
                     lambda e: e.dma_start(out=sA[:], in_=scaT),
                     lambda e: e.dma_start(out=sBs[:], in_=scbT),
                     lambda e: e.dma_start(out=sH[:], in_=shT)], "d_misc", [],
              [("vecs",), ("sA",), ("sB",), ("sH",)])
        P.op("pool", lambda e: e.memset(wrbd[:], 0.0), [], [("wrbd",)])
        P.op("pool", lambda e: e.memset(wibd[:], 0.0), [], [("wibd",)])
        P.op("pool", lambda e: e.memset(ones[:], 1.0 / D), [], [("ones",)])
        P.op("pool", lambda e: e.memset(sop[:], 0.0), [], [("sop", l, j) for l in range(L) for j in range(KC)])
        P.op("pool", lambda e: e.memset(dv[:, DV_EPS:DV_EPS + 1], EPS), [], [("dv",)])
        fns = []
        for (wsrc, wdst) in ((w_r, wrbd), (w_i, wibd)):
            for l in range(L):
                for half in range(2):
                    src = wsrc[l].rearrange("(j h) i o -> h i j o", h=2)[half]
                    dst = wdst[half * 64:(half + 1) * 64, l, :, half * 64:(half + 1) * 64]
                    fns.append(lambda e, s=src, d=dst: e.dma_start(out=d, in_=s))
        gate_fns = fns

        rv = [("vecs",)]
        wv = [("dv",)]
        ts(dv[:, DV_CAWH:DV_CAWH + 48], vecs[:, V_CAW:V_CAW + 48], 0.5, None, ALU.mult, None, rv, wv)
        ts(dv[:, DV_BRH:DV_BRH + 16], vecs[:, V_BR:V_BR + 16], 0.5, None, ALU.mult, None, rv, wv)
        ts(dv[:, DV_BIH:DV_BIH + 16], vecs[:, V_BI:V_BI + 16], 0.5, None, ALU.mult, None, rv, wv)
        lam = vecs[:, V_LAM:V_LAM + 16]
        t_abs = dv[:, DV_TMP:DV_TMP + 16]
        t_e = dv[:, DV_TMP + 16:DV_TMP + 32]
        t_m = dv[:, DV_TMP + 32:DV_TMP + 48]
        ts(t_m, lam, -1.0, None, ALU.mult, None, rv, wv)
        tt(t_abs, lam, t_m, ALU.max, rv + wv, wv)
        act(t_e, t_abs, AF.Exp, wv, wv, scale=-1.0)
        act(t_e, t_e, AF.Ln, wv, wv, bias=1.0)
        ts(t_m, t_m, 0.0, None, ALU.max, None, wv, wv)
        tt(t_m, t_m, t_e, ALU.add, wv, wv)
        ts(dv[:, DV_CH:DV_CH + 16], t_m, -4.0, None, ALU.mult, None, wv, wv)
        ts(dv[:, DV_CQ:DV_CQ + 16], t_m, -8.0, None, ALU.mult, None, wv, wv)
        ts(dv[:, DV_C4:DV_C4 + 16], t_m, -2.0, None, ALU.mult, None, wv, wv)
        CONST_R = [("vecs",), ("dv",)]

        units = []
        for g in range(2):
            for l in range(L):
                for j in range(KC):
                    units.append((g, l, "B", j))
                    units.append((g, l, "A", j))
                for kind in ("MG", "WO", "GT"):
                    for j in range(KC):
                        units.append((g, l, kind, j))
        def wsrc_cols(wl, c0):
            return wl[:, c0:c0 + 128].rearrange("(k p) n -> p k n", p=128)

        pending = []
        for ui_, (g, l, kind, j) in enumerate(units):
            wt = wring[ui_ % NSLOT]

            def add(dst_off, src, kk=KC, wt=wt, ui_=ui_):
                dst = wt[:, dst_off:dst_off + kk * 128].rearrange("p (k n) -> p k n", n=128)
                first = not pending or pending[-1][0] != ui_
                pending.append((ui_, (lambda e, s=src, d=dst: e.dma_start(out=d, in_=s)), first))

            if kind == "A":
                for c in range(4):
                    add(c * 1024, wsrc_cols(w_in[l], c * 1024 + j * 128))
            elif kind == "B":
                for c in range(2):
                    add(c * 1024, wsrc_cols(w_in[l], (4 + c) * 1024 + j * 128))
            elif kind == "MG":
                add(0, wsrc_cols(w_in[l], 6 * 1024 + j * 128))
                add(1024, wsrc_cols(w_in[l], 7 * 1024 + j * 128))
                add(2048, wsrc_cols(w_a_out[l], j * 128))
                add(3072, wsrc_cols(w_b_out[l], j * 128))
            elif kind == "WO":
                add(0, wsrc_cols(w_o[l], j * 128))
            else:
                add(0, wsrc_cols(w_pg[l], j * 128))
                add(1024, wsrc_cols(w_pe[l], j * 128), kk=2)
        pptr = [0]

        def pump(cur_ui, maxn, force_upto=-1):
            n = 0
            while pptr[0] < len(pending):
                uidx, fn, first = pending[pptr[0]]
                if uidx > cur_ui + NSLOT - 1:
                    break
                if uidx > force_upto and n >= maxn:
                    break
                slot = uidx % NSLOT
                P.dma("pool", [fn], f"d_w{slot}", [], [("w", slot)], nodeps=not first)
                pptr[0] += 1
                n += 1

        def wview(slot, off):
            return wring[slot][:, off:off + 128]

        def tiles_of(g):
            if g == 0:
                return [("p", 512, 0, 0), ("m", 512, 512, 512)]
            return [("p", 384, 0, 1024), ("p", 384, 384, 1408), ("p", 320, 768, 1792)]

        def norm_phase(g, gcol, out_hn, final=False, after_tile=None):
            for ti, (kind, n, col, gofs) in enumerate(tiles_of(g)):
                bk = bank()
                sqs = []
                for k in range(KC):
                    sq, sqk = wk("sq")
                    act(sq[:, 0:n], x[:, k, col:col + n], AF.Square, [("x", k, ti)], [sqk])
                    sqs.append((sq, sqk))
                    def fn(e, k=k, sq=sq, bk=bk, n=n):
                        return e.matmul(out=psum[bk][:, 0:n], lhsT=ones[:], rhs=sq[:, 0:n],
                                        start=(k == 0), stop=(k == KC - 1))
                    P.op("pe", fn, [sqk, ("ones",)], [("ps", bk)])
                rstd, rk = wk("xc")
                act(rstd[:, 0:n], psum[bk][:, 0:n], AF.Ln, [("ps", bk), ("dv",)], [rk], bias=dcol(DV_EPS))
                act(rstd[:, 0:n], rstd[:, 0:n], AF.Exp, [rk], [rk], scale=-0.5)
                for k in range(KC):
                    if final:
                        stt(x[:, k, col:col + n], x[:, k, col:col + n], vcol(gcol + k), rstd[:, 0:n],
                            ALU.mult, ALU.mult, [("x", k, ti), rk, ("vecs",)], [("x", k, ti)])
                    else:
                        stt(hn[:, k, col:col + n], x[:, k, col:col + n], vcol(gcol + k), rstd[:, 0:n],
                            ALU.mult, ALU.mult, [("x", k, ti), rk, ("vecs",)], [("hn", k, ti)])
                if after_tile is not None:
                    after_tile(ti)

        def ab_front(ui, g, l, j, ti, tile):
            slotB = ui % NSLOT
            slotA = (ui + 1) % NSLOT
            lj = l * KC + j
            IDN = AF.Identity
            kind, n, col, gofs = tile
            ns_ = NS if kind == "m" else 0
            np_ = n - ns_
            n_ptiles = len(tiles_of(g))
            hn_r = [("hn", k, ti) for k in range(KC)]

            def grp(c, bk=None):
                if bk is None:
                    bk = bank()
                slot, cc = (slotA, c) if c < 4 else (slotB, c - 4)
                mm(bk, n, [(wview(slot, cc * 1024 + k * 128), hn[:, k, col:col + n]) for k in range(KC)],
                   hn_r + [("w", slot)])
                return bk

            last_p = ti == n_ptiles - 1
            sopk = ("sop", l, j)
            sosk = ("sos", l, j)
            cbw = lambda kk: vcol(V_CBW + (l * 4 + kk) * 8 + j)
            caw = lambda kk: dcol(DV_CAWH + (l * 3 + kk) * 8 + j)
            bk0, bk1 = bank(), bank()
            it_par[0] ^= 1
            if it_par[0]:
                bk0, bk1 = bk1, bk0
            b_xb = grp(4, bk0)
            b_ua = grp(0, bk1)
            b_gb = grp(5)
            b_ga = grp(3)
            b_ca = grp(2)
            b_ba = grp(1)
            xc, xck = wk("xc")
            pxb = psum[b_xb]
            pk = ("ps", b_xb)
            act(xc[:, 0:n], pxb[:, 0:n], IDN, [pk, ("vecs",)], [xck], bias=vcol(V_CBB + lj), scale=cbw(3))
            if ti == 0:
                hist, histk = sop[:, l, j, 2:5], sopk
            else:
                hist, histk = hbprev[0], hbprev[1]
            if last_p:
                P.op("act", lambda e: e.activation(out=sop[:, l, j, 2:5], in_=pxb[:, np_ - 3:np_], func=AF.Copy),
                     [pk, histk], [sopk])
            else:
                hb, hbk = wk("hb")
                P.op("act", lambda e: e.activation(out=hb[:, 0:3], in_=pxb[:, np_ - 3:np_], func=AF.Copy),
                     [pk], [hbk])
                hbprev[0], hbprev[1] = hb[:, 0:3], hbk
            if ns_:
                p3 = v3(pxb[:, np_:n], ST)
                xc3 = v3(xc[:, np_:n], ST)
                xbS, xbSk = wk("xbS")
                xb3 = v3(xbS[:, 0:SB * 7], 7)
                P.op("act", lambda e: e.activation(out=xb3[:, :, 0:3], in_=sBs[:, l, j, :, :], func=AF.Copy),
                     [("sB",)], [xbSk])
                P.op("act", lambda e: e.activation(out=xb3[:, :, 3:7], in_=p3, func=AF.Copy), [pk], [xbSk])
                cp(sos[:, l, j, :, 2:5], xb3[:, :, 4:7], [xbSk], [sosk], eng="pool")
            cv, cvk = wk("cv")
            ua, uak = cv, cvk
            act(ua[:, 0:n], psum[b_ua][:, 0:n], AF.Copy, [("ps", b_ua)], [uak])
            s2, s2k = wk("s2")
            act(s2[:, 0:n], psum[b_gb][:, 0:n], AF.Tanh, [("ps", b_gb)], [s2k], scale=0.5)
            tg, tgk = wk("tg")
            act(tg[:, 0:n], psum[b_ga][:, 0:n], AF.Tanh, [("ps", b_ga)], [tgk], scale=0.5)
            for d in range(1, 4):
                w_ = cbw(3 - d)
                stt(xc[:, d:np_], pxb[:, 0:np_ - d], w_, xc[:, d:np_], ALU.mult, ALU.add, [pk, xck, ("vecs",)], [xck])
                if not (ti == 0 and g == 0):
                    stt(xc[:, 0:d], hist[:, 3 - d:3], w_, xc[:, 0:d], ALU.mult, ALU.add,
                        [histk, xck, ("vecs",)], [xck])
                if ns_:
                    stt(xc3, xb3[:, :, 3 - d:3 - d + ST], w_, xc3, ALU.mult, ALU.add, [xbSk, xck, ("vecs",)], [xck])
            xcb, xcbk = wk("xcb")
            act(xcb[:, 0:n], xc[:, 0:n], AF.Copy, [xck], [xcbk])
            pca = psum[b_ca]
            cu, cuk = wk("cu")
            if ti == 0:
                cp(cu[:, 0:2], sop[:, l, j, 0:2], [sopk], [cuk])
            tt(cu[:, 2:2 + np_], pca[:, 0:np_], ua[:, 0:np_], ALU.mult, [("ps", b_ca), uak], [cuk])
            if not last_p:
                cun, cunk = wk_peek("cu")
                cp(cun[:, 0:2], cu[:, np_:np_ + 2], [cuk], [cunk])
            if ns_:
                cuS, cuSk = wk("cuS")
                cu3 = v3(cuS[:, 0:SB * 6], 6)
                cp(cu3[:, :, 0:2], sA[:, l, j, :, :], [("sA",)], [cuSk])
                tt(cu3[:, :, 2:6], v3(pca[:, np_:n], ST), v3(ua[:, np_:n], ST), ALU.mult,
                   [("ps", b_ca), uak], [cuSk])
            ts(cv[:, 0:np_], cu[:, 0:np_], caw(0), None, ALU.mult, None, [cuk, ("dv",)], [cvk])
            if ns_:
                cv3 = v3(cv[:, np_:n], ST)
                act(cv3, cu3[:, :, 0:ST], IDN, [cuSk, ("dv",)], [cvk], scale=caw(0))
            stt(s2[:, 0:n], s2[:, 0:n], 1.0, psum[b_gb][:, 0:n], ALU.add, ALU.mult, [s2k, ("ps", b_gb)], [s2k])
            stt(tg[:, 0:n], tg[:, 0:n], 1.0, psum[b_ga][:, 0:n], ALU.add, ALU.mult, [tgk, ("ps", b_ga)], [tgk])
            tt(tg[:, 0:n], psum[b_ba][:, 0:n], tg[:, 0:n], ALU.mult, [("ps", b_ba), tgk], [tgk])
            for kk in range(1, 3):
                stt(cv[:, 0:np_], cu[:, kk:kk + np_], caw(kk), cv[:, 0:np_], ALU.mult, ALU.add,
                    [cuk, cvk, ("dv",)], [cvk])
                if ns_:
                    stt(cv3, cu3[:, :, kk:kk + ST], caw(kk), cv3, ALU.mult, ALU.add, [cuSk, cvk, ("dv",)], [cvk])
            if last_p:
                cp(sop[:, l, j, 0:2], cu[:, np_:np_ + 2], [cuk], [sopk], eng="pool")
            if ns_:
                cp(sos[:, l, j, :, 0:2], cu3[:, :, 4:6], [cuSk], [sosk], eng="pool")
            tt(ya[:, j, col:col + n], cv[:, 0:n], tg[:, 0:n], ALU.mult, [cvk, tgk], [("ya", j, ti)], eng="pool")
            return dict(g=g, l=l, j=j, ti=ti, tile=tile, xc=xc, xck=xck, xcb=xcb, xcbk=xcbk, s2=s2, s2k=s2k,
                        np_=np_, ns_=ns_, last_p=last_p)

        hprev = [None, None]
        it_par = [0]
        hbprev = [None, None]

        def ab_back(c):
            l, j, ti = c["l"], c["j"], c["ti"]
            kind, n, col, gofs = c["tile"]
            np_, ns_, last_p = c["np_"], c["ns_"], c["last_p"]
            xc, xck, xcb, xcbk, s2, s2k = c["xc"], c["xck"], c["xcb"], c["xcbk"], c["s2"], c["s2k"]
            lj = l * KC + j
            sopk = ("sop", l, j)
            sosk = ("sos", l, j)
            b_zr = bank()
            mm(b_zr, n, [(wrbd[:, l, j, :], xcb[:, 0:n])], [xcbk, ("wrbd",)])
            b_zi = bank()
            mm(b_zi, n, [(wibd[:, l, j, :], xcb[:, 0:n])], [xcbk, ("wibd",)])
            tr, trk = wk("tr")
            ti_, tik = wk("ti")
            act(tr[:, 0:n], psum[b_zr][:, 0:n], AF.Tanh, [("ps", b_zr), ("dv",)], [trk],
                bias=dcol(DV_BRH + lj), scale=0.5)
            act(ti_[:, 0:n], psum[b_zi][:, 0:n], AF.Tanh, [("ps", b_zi), ("dv",)], [tik],
                bias=dcol(DV_BIH + lj), scale=0.5)
            a_, ak = wk("a")
            th, thk = wk("th")
            act(a_[:, 0:n], tr[:, 0:n], AF.Exp, [trk, ("dv",)], [ak], bias=dcol(DV_CH + lj), scale=dcol(DV_CH + lj))
            if c["g"] == 0:
                act(th[:, 0:n], tr[:, 0:n], AF.Exp, [trk, ("dv",)], [thk], bias=dcol(DV_CQ + lj), scale=dcol(DV_CQ + lj))
                act(th[:, 0:n], th[:, 0:n], AF.Relu, [thk], [thk], bias=0.25, scale=-0.25)
                act(th[:, 0:n], th[:, 0:n], AF.Sqrt, [thk], [thk])
                stt(ti_[:, 0:n], ti_[:, 0:n], 1.0, xc[:, 0:n], ALU.add, ALU.mult, [tik, xck], [tik])
            else:
                act(th[:, 0:n], tr[:, 0:n], AF.Tanh, [trk, ("dv",)], [thk], bias=dcol(DV_C4 + lj), scale=dcol(DV_C4 + lj))
                act(th[:, 0:n], th[:, 0:n], AF.Sqrt, [thk], [thk], scale=-0.25)
                stt(ti_[:, 0:n], ti_[:, 0:n], 1.0, xc[:, 0:n], ALU.add, ALU.mult, [tik, xck], [tik])
                stt(th[:, 0:n], a_[:, 0:n], 1.0, th[:, 0:n], ALU.add, ALU.mult, [ak, thk], [thk])
            tt(ti_[:, 0:n], th[:, 0:n], ti_[:, 0:n], ALU.mult, [thk, tik], [tik])
            h, hk = wk("h")
            if ti == 0:
                init_ap, init_k = sop[:, l, j, 5:6], sopk
            else:
                init_ap, init_k = hprev[0], hprev[1]
            if ns_:
                a3 = v3(a_[:, np_:n], ST)
                u3 = v3(ti_[:, np_:n], ST)
                scr = h[:, np_:np_ + SB]
                tt(scr, a3[:, :, 0], sH[:, l, j, :], ALU.mult, [ak, ("sH",)], [hk])
                tt(u3[:, :, 0], u3[:, :, 0], scr, ALU.add, [tik, hk], [tik])
                P.op("dve", lambda e: e.memset(a3[:, :, 0], 0.0), [], [ak])
            P.op("dve", lambda e: e.tensor_tensor_scan(
                out=h[:, 0:n], data0=a_[:, 0:n], data1=ti_[:, 0:n], initial=init_ap,
                op0=ALU.mult, op1=ALU.add), [ak, tik, init_k], [hk])
            hprev[0], hprev[1] = h[:, np_ - 1:np_], hk
            if last_p:
                cp(sop[:, l, j, 5:6], h[:, np_ - 1:np_], [hk], [sopk], eng="pool")
            if ns_:
                h3 = v3(h[:, np_:n], ST)
                cp(sos[:, l, j, :, 5], h3[:, :, ST - 1], [hk], [sosk], eng="pool")
            tt(yb[:, j, col:col + n], h[:, 0:n], s2[:, 0:n], ALU.mult, [hk, s2k], [("yb", j, ti)], eng="pool")

        def mg_unit(ui, g, l, j, pumpf):
            slot = ui % NSLOT
            wkey = ("w", slot)
            for ti, (kind, n, col, gofs) in enumerate(tiles_of(g)):
                pumpf()
                def grp(off, src, skey):
                    bk = bank()
                    mm(bk, n, [(wview(slot, off + k * 128), src[:, k, col:col + n]) for k in range(KC)],
                       [(skey, k, ti) for k in range(KC)] + [wkey])
                    return bk
                b_ma = grp(0, hn, "hn")
                b_pa = grp(2048, ya, "ya")
                b_mb = grp(1024, hn, "hn")
                b_pb = grp(3072, yb, "yb")
                q1, q1k = wk("tg")
                q2, q2k = wk("cv")
                act(q1[:, 0:n], psum[b_ma][:, 0:n], AF.Tanh, [("ps", b_ma)], [q1k], scale=0.5)
                stt(q1[:, 0:n], q1[:, 0:n], 1.0, psum[b_pa][:, 0:n], ALU.add, ALU.mult, [q1k, ("ps", b_pa)], [q1k])
                act(q2[:, 0:n], psum[b_mb][:, 0:n], AF.Tanh, [("ps", b_mb)], [q2k], scale=0.5)
                stt(q2[:, 0:n], q2[:, 0:n], 1.0, psum[b_pb][:, 0:n], ALU.add, ALU.mult, [q2k, ("ps", b_pb)], [q2k])
                stt(mg[:, j, col:col + n], q2[:, 0:n], 0.5, q1[:, 0:n], ALU.mult, ALU.add, [q1k, q2k], [("mg", j, ti)])

        def wo_unit(ui, g, l, j, pumpf):
            slot = ui % NSLOT
            wkey = ("w", slot)
            for ti, (kind, n, col, gofs) in enumerate(tiles_of(g)):
                pumpf()
                bk = bank()
                mm(bk, n, [(wview(slot, k * 128), mg[:, k, col:col + n]) for k in range(KC)],
                   [("mg", k, ti) for k in range(KC)] + [wkey])
                stt(x[:, j, col:col + n], psum[bk][:, 0:n], 0.5, x[:, j, col:col + n], ALU.mult, ALU.add,
                    [("ps", bk), ("x", j, ti)], [("x", j, ti)])

        def gt_unit(ui, g, l, j, pumpf):
            slot = ui % NSLOT
            wkey = ("w", slot)
            for ti, (kind, n, col, gofs) in enumerate(tiles_of(g)):
                pumpf()
                b_pg = bank()
                mm(b_pg, n, [(wview(slot, k * 128), hn[:, k, col:col + n]) for k in range(KC)],
                   [("hn", k, ti) for k in range(KC)] + [wkey])
                b_pe = bank()
                mm(b_pe, n, [(wview(slot, 1024 + k * 128), pT[:, k, col:col + n]) for k in range(2)],
                   [("pT", ti), wkey])
                q1, q1k = wk("tg")
                act(q1[:, 0:n], psum[b_pg][:, 0:n], AF.Tanh, [("ps", b_pg)], [q1k], scale=0.5)
                stt(q1[:, 0:n], q1[:, 0:n], 1.0, psum[b_pe][:, 0:n], ALU.add, ALU.mult, [q1k, ("ps", b_pe)], [q1k])
                stt(x[:, j, col:col + n], q1[:, 0:n], 0.5, x[:, j, col:col + n], ALU.mult, ALU.add,
                    [q1k, ("x", j, ti)], [("x", j, ti)])

        ui = 0
        prev_stores = []
        for g in range(2):
            tl = tiles_of(g)
            for ti, (kind, n, col, gofs) in enumerate(tl):
                extra = [sv for (c0, c1, sv) in prev_stores if c0 < col + n and col < c1]
                P.dma("sp", [lambda e, col=col, n=n, gofs=gofs: e.dma_start(out=x[:, :, col:col + n],
                                                                        in_=xin[:, :, gofs:gofs + n])],
                      f"d_x{ti}", [], [("x", k, ti) for k in range(KC)], extra=extra)
            for l in range(L):
                pump(ui, 0, force_upto=ui + 1)
                if gate_fns:
                    P.dma("pool", gate_fns, "d_misc2", [], [("wrbd",), ("wibd",)])
                    gate_fns = None
                P.dma("pool", [lambda e, col=col, n=n, gofs=gofs, l=l: e.dma_start(
                    out=pT[:, :, col:col + n], in_=pin[l, :, :, gofs:gofs + n])
                    for (kind, n, col, gofs) in tl],
                      "d_p", [], [("pT", ti) for ti in range(len(tl))])
                norm_phase(g, V_NIN + l * 8, hn)
                pend = None
                for j in range(KC):
                    assert units[ui] == (g, l, "B", j) and units[ui + 1] == (g, l, "A", j)
                    pump(ui, 0, force_upto=ui + 1)
                    for ti, tile in enumerate(tl):
                        pump(ui, 99)
                        c = ab_front(ui, g, l, j, ti, tile)
                        if pend is not None:
                            ab_back(pend)
                        pend = c
                    ui += 2
                ab_back(pend)
                for kind_u, fn_u in (("MG", mg_unit), ("WO", wo_unit)):
                    for j in range(KC):
                        assert units[ui] == (g, l, kind_u, j)
                        pump(ui, 0, force_upto=ui)
                        fn_u(ui, g, l, j, lambda ui=ui: pump(ui, 99))
                        ui += 1
                norm_phase(g, V_NPE + l * 8, hn)
                for j in range(KC):
                    assert units[ui] == (g, l, "GT", j)
                    pump(ui, 0, force_upto=ui)
                    gt_unit(ui, g, l, j, lambda ui=ui: pump(ui, 99))
                    ui += 1
            stores = []

            def store_tile(ti, tl=tl, stores=stores):
                kind, n, col, gofs = tl[ti]
                P.dma("sp", [lambda e: e.dma_start(out=yT[:, :, gofs:gofs + n], in_=x[:, :, col:col + n])],
                      f"d_y{ti}", [("x", k, ti) for k in range(KC)], [])
                stores.append((col, col + n, (f"d_y{ti}", P.cnt[f"d_y{ti}"])))

            norm_phase(g, V_NF, None, final=True, after_tile=store_tile)
            prev_stores = stores
        sop_keys = [("sop", l, j) for l in range(L) for j in range(KC)]
        sos_keys = [("sos", l, j) for l in range(L) for j in range(KC)]
        P.dma("sp", [lambda e: e.dma_start(out=sop_d, in_=sop[:])], "d_so", sop_keys, [])
        P.dma("sp", [lambda e: e.dma_start(out=sos_d, in_=sos[:])], "d_so", sos_keys, [])
        P.final_wait("sp", [f"d_y{i}" for i in range(3)] + ["d_so"])

        with nc.Block() as block:
            def replay(name):
                fuse = name in ("act", "dve", "pool")

                def run(e):
                    pend = []
                    for item in P.stream[name]:
                        if item[0] == "wait":
                            if fuse:
                                pend.append(item)
                            else:
                                e.wait_ge(SEM[item[1]], item[2])
                        else:
                            if item[3] != 1:
                                for w in pend:
                                    e.wait_ge(SEM[w[1]], w[2])
                                pend = []
                            for w in pend[:-1]:
                                e.wait_ge(SEM[w[1]], w[2])
                            ins = item[1](e)
                            if pend:
                                ins._wait_ge(SEM[pend[-1][1]], pend[-1][2])
                            pend = []
                            ins.then_inc(SEM[item[2]], item[3])
                    for w in pend:
                        e.wait_ge(SEM[w[1]], w[2])
                return run

            block.tensor(replay("pe"))
            block.scalar(replay("act"))
            block.vector(replay("dve"))
            block.gpsimd(replay("pool"))
            block.sync(replay("sp"))
    return nc


def _fm(v):
    v = np.asarray(v, np.float32)
    lead = v.shape[:-1]
    r = v.reshape(lead + (KC, 128))
    return np.ascontiguousarray(np.moveaxis(r, -1, 0))


_NC_CACHE = {}


def kernel(x_prompt, x_sample, state_conv_a, state_conv_b, state_h, p_prompt, p_sample,
           norm_in, w_in, conv_a_w, conv_b_w, conv_b_b, w_r, b_r, w_i, b_i, lam,
           w_a_out, w_b_out, w_o, norm_pe, w_pg, w_pe, norm_final):
    f32 = np.float32
    vecs = np.zeros((128, NV), f32)
    vecs[:, V_NIN:V_NIN + 16] = _fm(norm_in).reshape(128, 16)
    vecs[:, V_CAW:V_CAW + 48] = _fm(conv_a_w).reshape(128, 48)
    vecs[:, V_CBW:V_CBW + 64] = _fm(conv_b_w).reshape(128, 64)
    vecs[:, V_CBB:V_CBB + 16] = _fm(conv_b_b).reshape(128, 16)
    vecs[:, V_BR:V_BR + 16] = _fm(b_r).reshape(128, 16)
    vecs[:, V_BI:V_BI + 16] = _fm(b_i).reshape(128, 16)
    vecs[:, V_LAM:V_LAM + 16] = _fm(lam).reshape(128, 16)
    vecs[:, V_NPE:V_NPE + 16] = _fm(norm_pe).reshape(128, 16)
    vecs[:, V_NF:V_NF + 8] = _fm(norm_final).reshape(128, 8)

    shared = dict(
        vecs=vecs,
        w_in=np.ascontiguousarray(w_in, f32), w_r=np.ascontiguousarray(w_r, f32),
        w_i=np.ascontiguousarray(w_i, f32), w_a_out=np.ascontiguousarray(w_a_out, f32),
        w_b_out=np.ascontiguousarray(w_b_out, f32), w_o=np.ascontiguousarray(w_o, f32),
        w_pg=np.ascontiguousarray(w_pg, f32), w_pe=np.ascontiguousarray(w_pe, f32),
    )
    in_maps = []
    for c in range(NCORES):
        ss = slice(c * SB, (c + 1) * SB)
        xp = np.asarray(x_prompt[c], f32)
        xs = np.asarray(x_sample[ss], f32).reshape(NS, D)
        toks = np.concatenate([xp[:PSPLIT], xs, xp[PSPLIT:]], axis=0)
        xin = np.ascontiguousarray(toks.reshape(NTOK, KC, 128).transpose(2, 1, 0))
        pin = np.empty((L, 128, 2, NTOK), f32)
        for l in range(L):
            pp = np.asarray(p_prompt[l, c], f32)
            ps_ = np.asarray(p_sample[l, ss], f32).reshape(NS, PLE)
            pt = np.concatenate([pp[:PSPLIT], ps_, pp[PSPLIT:]], axis=0)
            pin[l] = pt.reshape(NTOK, 2, 128).transpose(2, 1, 0)
        sca = np.asarray(state_conv_a[:, ss], f32).reshape(L, SB, 2, KC, 128).transpose(4, 0, 3, 1, 2)
        scb = np.asarray(state_conv_b[:, ss], f32).reshape(L, SB, 3, KC, 128).transpose(4, 0, 3, 1, 2)
        sh = np.asarray(state_h[:, ss], f32).reshape(L, SB, KC, 128).transpose(3, 0, 2, 1)
        m = dict(shared)
        m.update(xin=xin, pin=pin, scaT=np.ascontiguousarray(sca), scbT=np.ascontiguousarray(scb),
                 shT=np.ascontiguousarray(sh))
        in_maps.append(m)

    if "nc" not in _NC_CACHE:
        _NC_CACHE["nc"] = build_nc()
    nc = _NC_CACHE["nc"]
    res = run_bass_kernel_spmd(nc, in_maps, core_ids=list(range(NCORES)))

    y_prompt = np.empty((NCORES, SEQ, D), f32)
    y_sample = np.empty((NCORES * SB, ST, D), f32)
    ca_p = np.empty((L, NCORES, 2, D), f32)
    cb_p = np.empty((L, NCORES, 3, D), f32)
    h_p = np.empty((L, NCORES, D), f32)
    ca_s = np.empty((L, NCORES * SB, 2, D), f32)
    cb_s = np.empty((L, NCORES * SB, 3, D), f32)
    h_s = np.empty((L, NCORES * SB, D), f32)
    for c in range(NCORES):
        r = res.results[c]
        ss = slice(c * SB, (c + 1) * SB)
        yt = np.asarray(r["yT"]).transpose(2, 1, 0).reshape(NTOK, D)
        y_prompt[c, :PSPLIT] = yt[:PSPLIT]
        y_prompt[c, PSPLIT:] = yt[PSPLIT + NS:]
        y_sample[ss] = yt[PSPLIT:PSPLIT + NS].reshape(SB, ST, D)
        sop = np.asarray(r["sop"])
        so = sop.transpose(1, 3, 2, 0).reshape(L, 6, D)
        ca_p[:, c] = so[:, 0:2]
        cb_p[:, c] = so[:, 2:5]
        h_p[:, c] = so[:, 5]
        sos = np.asarray(r["sos"])
        s2 = sos.transpose(1, 3, 4, 2, 0).reshape(L, SB, 6, D)
        ca_s[:, ss] = s2[:, :, 0:2]
        cb_s[:, ss] = s2[:, :, 2:5]
        h_s[:, ss] = s2[:, :, 5]
    return (y_prompt, y_sample, ca_p, cb_p, h_p, ca_s, cb_s, h_s)
```

```python
import numpy as np
import concourse.bass as bass
import concourse.mybir as mybir
from concourse.bass_utils import run_bass_kernel_spmd

F32 = mybir.dt.float32
BF16 = mybir.dt.bfloat16
AF = mybir.ActivationFunctionType
ALU = mybir.AluOpType

D = 1024
KC = 8
L = 2
NCORES = 8
SEQ = 2048
SB = 16
ST = 4
NS = SB * ST
NTOK = SEQ + NS
NG = 1024 + NS
PSPLIT = 960
PLE = 256
INC = 8192
EPS = 1e-6
NSLOT = 5
SLOT_ELEMS = 4 * 1024

V_NIN = 0
V_CAW = 16
V_CBW = 64
V_CBB = 128
V_BR = 144
V_BI = 160
V_LAM = 176
V_NPE = 192
V_NF = 208
NV = 216
DV_CAWH = 0
DV_BRH = 48
DV_BIH = 64
DV_CH = 80
DV_CQ = 96
DV_EPS = 112
DV_TMP = 113
DV_C4 = 164
NDV = 200


class Prog:
    ENG = ("pe", "act", "dve", "pool", "sp")

    def __init__(self):
        self.stream = {e: [] for e in self.ENG}
        self.cnt = {}
        self.waited = {e: {} for e in self.ENG}
        self.lw = {}
        self.rd = {}

    def _deps(self, eng, reads, writes):
        need = {}

        def add(sv):
            s, v = sv
            if s == "pe" and eng == "pe":
                return
            if need.get(s, 0) < v:
                need[s] = v

        for k in reads:
            w = self.lw.get(k)
            if w:
                add(w)
        for k in writes:
            w = self.lw.get(k)
            if w:
                add(w)
            for r in self.rd.get(k, ()):
                add(r)
        for s, v in need.items():
            if self.waited[eng].get(s, 0) < v:
                self.waited[eng][s] = v
                self.stream[eng].append(("wait", s, v))

    def _record(self, sv, reads, writes):
        for k in writes:
            self.lw[k] = sv
            self.rd[k] = []
        for k in reads:
            self.rd.setdefault(k, []).append(sv)

    def op(self, eng, fn, reads=(), writes=()):
        if eng != "pe":
            psr = [k for k in reads if k[0] == "ps"]
            if psr:
                writes = list(writes) + psr
        self._deps(eng, reads, writes)
        v = self.cnt.get(eng, 0) + 1
        self.cnt[eng] = v
        self.stream[eng].append(("op", fn, eng, 1))
        self._record((eng, v), reads, writes)

    def dma(self, eng, fns, dsem, reads=(), writes=(), nodeps=False, extra=()):
        if not nodeps:
            self._deps(eng, reads, writes)
        for s_, v_ in extra:
            if self.waited[eng].get(s_, 0) < v_:
                self.waited[eng][s_] = v_
                self.stream[eng].append(("wait", s_, v_))
        for fn in fns:
            self.cnt[dsem] = self.cnt.get(dsem, 0) + 16
            self.stream[eng].append(("op", fn, dsem, 16))
        self._record((dsem, self.cnt[dsem]), reads, writes)

    def final_wait(self, eng, sems):
        for s in sems:
            v = self.cnt.get(s, 0)
            if v and self.waited[eng].get(s, 0) < v:
                self.waited[eng][s] = v
                self.stream[eng].append(("wait", s, v))


def build_nc():
    nc = bass.Bass("TRN2", target_bir_lowering=False)

    def din(name, shape):
        return nc.dram_tensor(name, shape, F32, kind="ExternalInput").ap()

    def dout(name, shape):
        return nc.dram_tensor(name, shape, F32, kind="ExternalOutput").ap()

    xin = din("xin", [128, KC, NTOK])
    pin = din("pin", [L, 128, 2, NTOK])
    scaT = din("scaT", [128, L, KC, SB, 2])
    scbT = din("scbT", [128, L, KC, SB, 3])
    shT = din("shT", [128, L, KC, SB])
    vecs_d = din("vecs", [128, NV])
    w_in = din("w_in", [L, D, INC])
    w_r = din("w_r", [L, 16, 64, 64])
    w_i = din("w_i", [L, 16, 64, 64])
    w_a_out = din("w_a_out", [L, D, D])
    w_b_out = din("w_b_out", [L, D, D])
    w_o = din("w_o", [L, D, D])
    w_pg = din("w_pg", [L, D, D])
    w_pe = din("w_pe", [L, PLE, D])
    yT = dout("yT", [128, KC, NTOK])
    sop_d = dout("sop", [128, L, KC, 6])
    sos_d = dout("sos", [128, L, KC, SB, 6])

    P = Prog()
    from contextlib import ExitStack
    with ExitStack() as ctx:
        def sb(name, shape, dt):
            return ctx.enter_context(nc.sbuf_tensor(name, shape, dt))

        x = sb("x", [128, KC, NG], F32)
        hn = sb("hn", [128, KC, NG], BF16)
        ya = sb("ya", [128, KC, NG], BF16)
        yb = sb("yb", [128, KC, NG], BF16)
        mg = sb("mg", [128, KC, NG], BF16)
        pT = sb("pT", [128, 2, NG], BF16)
        wring = [sb(f"wslot{i}", [128, SLOT_ELEMS], BF16) for i in range(NSLOT)]
        wrbd = sb("wrbd", [128, L, KC, 128], BF16)
        wibd = sb("wibd", [128, L, KC, 128], BF16)
        vecs = sb("vecs_sb", [128, NV], F32)
        dv = sb("dv", [128, NDV], F32)
        sA = sb("sA", [128, L, KC, SB, 2], F32)
        sBs = sb("sB", [128, L, KC, SB, 3], F32)
        sH = sb("sH", [128, L, KC, SB], F32)
        sop = sb("sop_sb", [128, L, KC, 6], F32)
        sos = sb("sos_sb", [128, L, KC, SB, 6], F32)
        ones = sb("ones", [128, 128], BF16)

        WK = {}
        wk_idx = {}

        def mkwork(name, cols, dt, nbuf):
            WK[name] = [sb(f"wk_{name}{i}", [128, cols], dt) for i in range(nbuf)]
            wk_idx[name] = 0

        def wk(name):
            i = wk_idx[name]
            wk_idx[name] = (i + 1) % len(WK[name])
            return WK[name][i], ("wk", name, i)

        def wk_peek(name):
            i = wk_idx[name]
            return WK[name][i], ("wk", name, i)

        mkwork("sq", 512, BF16, 3)
        mkwork("cu", 2 + 512, F32, 2)
        mkwork("cuS", SB * 6, F32, 1)
        mkwork("cv", 512, F32, 2)
        mkwork("tg", 512, F32, 2)
        mkwork("hb", 4, F32, 2)
        mkwork("xbS", SB * 7, F32, 2)
        mkwork("xc", 512, F32, 2)
        mkwork("xcb", 512, BF16, 2)
        mkwork("tr", 512, F32, 1)
        mkwork("ti", 512, F32, 1)
        mkwork("a", 512, F32, 1)
        mkwork("th", 512, F32, 1)
        mkwork("h", 512, F32, 2)
        mkwork("s2", 512, F32, 2)

        psum = [ctx.enter_context(nc.psum_tensor(f"ps{i}", [128, 512], F32)) for i in range(8)]
        ps_next = [0]

        def bank():
            b = ps_next[0]
            ps_next[0] = (b + 1) % 8
            return b

        sem_names = list(Prog.ENG) + [f"d_w{i}" for i in range(NSLOT)] + [f"d_x{i}" for i in range(3)] + [f"d_y{i}" for i in range(3)] + ["d_p", "d_misc", "d_misc2", "d_so"]
        SEM = {n: ctx.enter_context(nc.semaphore("sem_" + n)) for n in sem_names}

        def vcol(c):
            return vecs[:, c:c + 1]

        def dcol(c):
            return dv[:, c:c + 1]

        def act(out, in_, func, reads, writes, bias=None, scale=None):
            kw = {}
            if bias is not None:
                kw["bias"] = bias
            if scale is not None:
                kw["scale"] = scale
            P.op("act", lambda e: e.activation(out=out, in_=in_, func=func, **kw), reads, writes)

        def tt(out, in0, in1, op, reads, writes, eng="dve"):
            P.op(eng, lambda e: e.tensor_tensor(out=out, in0=in0, in1=in1, op=op), reads, writes)

        def ts(out, in0, s1, s2, op0, op1, reads, writes, eng="dve"):
            if s2 is None:
                P.op(eng, lambda e: e.tensor_scalar(out=out, in0=in0, scalar1=s1, scalar2=None, op0=op0),
                     reads, writes)
            else:
                P.op(eng, lambda e: e.tensor_scalar(out=out, in0=in0, scalar1=s1, scalar2=s2, op0=op0, op1=op1),
                     reads, writes)

        def stt(out, in0, scalar, in1, op0, op1, reads, writes):
            P.op("dve", lambda e: e.scalar_tensor_tensor(out=out, in0=in0, scalar=scalar, in1=in1, op0=op0, op1=op1),
                 reads, writes)

        def cp(out, in_, reads, writes, eng="dve"):
            P.op(eng, lambda e: e.tensor_copy(out=out, in_=in_), reads, writes)

        def mm(bk, n, pairs, reads):
            def fn(e):
                ins = None
                for i, (lh, rh) in enumerate(pairs):
                    ins = e.matmul(out=psum[bk][:, 0:n], lhsT=lh, rhs=rh, start=(i == 0), stop=(i == len(pairs) - 1))
                return ins
            P.op("pe", fn, reads, [("ps", bk)])

        def v3(ap, t):
            return ap.rearrange("p (s t) -> p s t", t=t)

        P.dma("sp", [lambda e: e.dma_start(out=vecs[:], in_=vecs_d),
                     lambda e: e.dma_start(out=sA[:], in_=scaT),
                     lambda e: e.dma_start(out=sBs[:], in_=scbT),
                     lambda e: e.dma_start(out=sH[:], in_=shT)], "d_misc", [],
              [("vecs",), ("sA",), ("sB",), ("sH",)])
        P.op("pool", lambda e: e.memset(wrbd[:], 0.0), [], [("wrbd",)])
        P.op("pool", lambda e: e.memset(wibd[:], 0.0), [], [("wibd",)])
        P.op("pool", lambda e: e.memset(ones[:], 1.0 / D), [], [("ones",)])
        P.op("pool", lambda e: e.memset(sop[:], 0.0), [], [("sop", l, j) for l in range(L) for j in range(KC)])
        P.op("pool", lambda e: e.memset(dv[:, DV_EPS:DV_EPS + 1], EPS), [], [("dv",)])
        fns = []
        for (wsrc, wdst) in ((w_r, wrbd), (w_i, wibd)):
            for l in range(L):
                for half in range(2):
                    src = wsrc[l].rearrange("(j h) i o -> h i j o", h=2)[half]
                    dst = wdst[half * 64:(half + 1) * 64, l, :, half * 64:(half + 1) * 64]
                    fns.append(lambda e, s=src, d=dst: e.dma_start(out=d, in_=s))
        gate_fns = fns

        rv = [("vecs",)]
        wv = [("dv",)]
        ts(dv[:, DV_CAWH:DV_CAWH + 48], vecs[:, V_CAW:V_CAW + 48], 0.5, None, ALU.mult, None, rv, wv)
        ts(dv[:, DV_BRH:DV_BRH + 16], vecs[:, V_BR:V_BR + 16], 0.5, None, ALU.mult, None, rv, wv)
        ts(dv[:, DV_BIH:DV_BIH + 16], vecs[:, V_BI:V_BI + 16], 0.5, None, ALU.mult, None, rv, wv)
        lam = vecs[:, V_LAM:V_LAM + 16]
        t_abs = dv[:, DV_TMP:DV_TMP + 16]
        t_e = dv[:, DV_TMP + 16:DV_TMP + 32]
        t_m = dv[:, DV_TMP + 32:DV_TMP + 48]
        ts(t_m, lam, -1.0, None, ALU.mult, None, rv, wv)
        tt(t_abs, lam, t_m, ALU.max, rv + wv, wv)
        act(t_e, t_abs, AF.Exp, wv, wv, scale=-1.0)
        act(t_e, t_e, AF.Ln, wv, wv, bias=1.0)
        ts(t_m, t_m, 0.0, None, ALU.max, None, wv, wv)
        tt(t_m, t_m, t_e, ALU.add, wv, wv)
        ts(dv[:, DV_CH:DV_CH + 16], t_m, -4.0, None, ALU.mult, None, wv, wv)
        ts(dv[:, DV_CQ:DV_CQ + 16], t_m, -8.0, None, ALU.mult, None, wv, wv)
        ts(dv[:, DV_C4:DV_C4 + 16], t_m, -2.0, None, ALU.mult, None, wv, wv)
        CONST_R = [("vecs",), ("dv",)]

        units = []
        for g in range(2):
            for l in range(L):
                for j in range(KC):
                    units.append((g, l, "B", j))
                    units.append((g, l, "A", j))
                for kind in ("MG", "WO", "GT"):
                    for j in range(KC):
                        units.append((g, l, kind, j))
        def wsrc_cols(wl, c0):
            return wl[:, c0:c0 + 128].rearrange("(k p) n -> p k n", p=128)

        pending = []
        for ui_, (g, l, kind, j) in enumerate(units):
            wt = wring[ui_ % NSLOT]

            def add(dst_off, src, kk=KC, wt=wt, ui_=ui_):
                dst = wt[:, dst_off:dst_off + kk * 128].rearrange("p (k n) -> p k n", n=128)
                first = not pending or pending[-1][0] != ui_
                pending.append((ui_, (lambda e, s=src, d=dst: e.dma_start(out=d, in_=s)), first))

            if kind == "A":
                for c in range(4):
                    add(c * 1024, wsrc_cols(w_in[l], c * 1024 + j * 128))
            elif kind == "B":
                for c in range(2):
                    add(c * 1024, wsrc_cols(w_in[l], (4 + c) * 1024 + j * 128))
            elif kind == "MG":
                add(0, wsrc_cols(w_in[l], 6 * 1024 + j * 128))
                add(1024, wsrc_cols(w_in[l], 7 * 1024 + j * 128))
                add(2048, wsrc_cols(w_a_out[l], j * 128))
                add(3072, wsrc_cols(w_b_out[l], j * 128))
            elif kind == "WO":
                add(0, wsrc_cols(w_o[l], j * 128))
            else:
                add(0, wsrc_cols(w_pg[l], j * 128))
                add(1024, wsrc_cols(w_pe[l], j * 128), kk=2)
        pptr = [0]

        def pump(cur_ui, maxn, force_upto=-1):
            n = 0
            while pptr[0] < len(pending):
                uidx, fn, first = pending[pptr[0]]
                if uidx > cur_ui + NSLOT - 1:
                    break
                if uidx > force_upto and n >= maxn:
                    break
                slot = uidx % NSLOT
                P.dma("pool", [fn], f"d_w{slot}", [], [("w", slot)], nodeps=not first)
                pptr[0] += 1
                n += 1

        def wview(slot, off):
            return wring[slot][:, off:off + 128]

        def tiles_of(g):
            if g == 0:
                return [("p", 512, 0, 0), ("m", 512, 512, 512)]
            return [("p", 384, 0, 1024), ("p", 384, 384, 1408), ("p", 320, 768, 1792)]

        def norm_phase(g, gcol, out_hn, final=False, after_tile=None):
            for ti, (kind, n, col, gofs) in enumerate(tiles_of(g)):
                bk = bank()
                sqs = []
                for k in range(KC):
                    sq, sqk = wk("sq")
                    act(sq[:, 0:n], x[:, k, col:col + n], AF.Square, [("x", k, ti)], [sqk])
                    sqs.append((sq, sqk))
                    def fn(e, k=k, sq=sq, bk=bk, n=n):
                        return e.matmul(out=psum[bk][:, 0:n], lhsT=ones[:], rhs=sq[:, 0:n],
                                        start=(k == 0), stop=(k == KC - 1))
                    P.op("pe", fn, [sqk, ("ones",)], [("ps", bk)])
                rstd, rk = wk("xc")
                act(rstd[:, 0:n], psum[bk][:, 0:n], AF.Ln, [("ps", bk), ("dv",)], [rk], bias=dcol(DV_EPS))
                act(rstd[:, 0:n], rstd[:, 0:n], AF.Exp, [rk], [rk], scale=-0.5)
                for k in range(KC):
                    if final:
                        stt(x[:, k, col:col + n], x[:, k, col:col + n], vcol(gcol + k), rstd[:, 0:n],
                            ALU.mult, ALU.mult, [("x", k, ti), rk, ("vecs",)], [("x", k, ti)])
                    else:
                        stt(hn[:, k, col:col + n], x[:, k, col:col + n], vcol(gcol + k), rstd[:, 0:n],
                            ALU.mult, ALU.mult, [("x", k, ti), rk, ("vecs",)], [("hn", k, ti)])
                if after_tile is not None:
                    after_tile(ti)

        def ab_front(ui, g, l, j, ti, tile):
            slotB = ui % NSLOT
            slotA = (ui + 1) % NSLOT
            lj = l * KC + j
            IDN = AF.Identity
            kind, n, col, gofs = tile
            ns_ = NS if kind == "m" else 0
            np_ = n - ns_
            n_ptiles = len(tiles_of(g))
            hn_r = [("hn", k, ti) for k in range(KC)]

            def grp(c):
                bk = bank()
                slot, cc = (slotA, c) if c < 4 else (slotB, c - 4)
                mm(bk, n, [(wview(slot, cc * 1024 + k * 128), hn[:, k, col:col + n]) for k in range(KC)],
                   hn_r + [("w", slot)])
                return bk

            last_p = ti == n_ptiles - 1
            sopk = ("sop", l, j)
            sosk = ("sos", l, j)
            cbw = lambda kk: vcol(V_CBW + (l * 4 + kk) * 8 + j)
            caw = lambda kk: dcol(DV_CAWH + (l * 3 + kk) * 8 + j)
            b_xb = grp(4)
            b_ua = grp(0)
            b_gb = grp(5)
            b_ga = grp(3)
            b_ca = grp(2)
            b_ba = grp(1)
            xc, xck = wk("xc")
            pxb = psum[b_xb]
            pk = ("ps", b_xb)
            act(xc[:, 0:n], pxb[:, 0:n], IDN, [pk, ("vecs",)], [xck], bias=vcol(V_CBB + lj), scale=cbw(3))
            if ti == 0:
                hist, histk = sop[:, l, j, 2:5], sopk
            else:
                hist, histk = hbprev[0], hbprev[1]
            if last_p:
                P.op("act", lambda e: e.activation(out=sop[:, l, j, 2:5], in_=pxb[:, np_ - 3:np_], func=AF.Copy),
                     [pk, histk], [sopk])
            else:
                hb, hbk = wk("hb")
                P.op("act", lambda e: e.activation(out=hb[:, 0:3], in_=pxb[:, np_ - 3:np_], func=AF.Copy),
                     [pk], [hbk])
                hbprev[0], hbprev[1] = hb[:, 0:3], hbk
            if ns_:
                p3 = v3(pxb[:, np_:n], ST)
                xc3 = v3(xc[:, np_:n], ST)
                xbS, xbSk = wk("xbS")
                xb3 = v3(xbS[:, 0:SB * 7], 7)
                P.op("act", lambda e: e.activation(out=xb3[:, :, 0:3], in_=sBs[:, l, j, :, :], func=AF.Copy),
                     [("sB",)], [xbSk])
                P.op("act", lambda e: e.activation(out=xb3[:, :, 3:7], in_=p3, func=AF.Copy), [pk], [xbSk])
                cp(sos[:, l, j, :, 2:5], xb3[:, :, 4:7], [xbSk], [sosk], eng="pool")
            cv, cvk = wk("cv")
            ua, uak = cv, cvk
            act(ua[:, 0:n], psum[b_ua][:, 0:n], AF.Copy, [("ps", b_ua)], [uak])
            s2, s2k = wk("s2")
            act(s2[:, 0:n], psum[b_gb][:, 0:n], AF.Tanh, [("ps", b_gb)], [s2k], scale=0.5)
            tg, tgk = wk("tg")
            act(tg[:, 0:n], psum[b_ga][:, 0:n], AF.Tanh, [("ps", b_ga)], [tgk], scale=0.5)
            for d in range(1, 4):
                w_ = cbw(3 - d)
                stt(xc[:, d:np_], pxb[:, 0:np_ - d], w_, xc[:, d:np_], ALU.mult, ALU.add, [pk, xck, ("vecs",)], [xck])
                if not (ti == 0 and g == 0):
                    stt(xc[:, 0:d], hist[:, 3 - d:3], w_, xc[:, 0:d], ALU.mult, ALU.add,
                        [histk, xck, ("vecs",)], [xck])
                if ns_:
                    stt(xc3, xb3[:, :, 3 - d:3 - d + ST], w_, xc3, ALU.mult, ALU.add, [xbSk, xck, ("vecs",)], [xck])
            xcb, xcbk = wk("xcb")
            act(xcb[:, 0:n], xc[:, 0:n], AF.Copy, [xck], [xcbk])
            pca = psum[b_ca]
            cu, cuk = wk("cu")
            if ti == 0:
                cp(cu[:, 0:2], sop[:, l, j, 0:2], [sopk], [cuk])
            tt(cu[:, 2:2 + np_], pca[:, 0:np_], ua[:, 0:np_], ALU.mult, [("ps", b_ca), uak], [cuk])
            if not last_p:
                cun, cunk = wk_peek("cu")
                cp(cun[:, 0:2], cu[:, np_:np_ + 2], [cuk], [cunk])
            if ns_:
                cuS, cuSk = wk("cuS")
                cu3 = v3(cuS[:, 0:SB * 6], 6)
                cp(cu3[:, :, 0:2], sA[:, l, j, :, :], [("sA",)], [cuSk])
                tt(cu3[:, :, 2:6], v3(pca[:, np_:n], ST), v3(ua[:, np_:n], ST), ALU.mult,
                   [("ps", b_ca), uak], [cuSk])
            ts(cv[:, 0:np_], cu[:, 0:np_], caw(0), None, ALU.mult, None, [cuk, ("dv",)], [cvk])
            if ns_:
                cv3 = v3(cv[:, np_:n], ST)
                act(cv3, cu3[:, :, 0:ST], IDN, [cuSk, ("dv",)], [cvk], scale=caw(0))
            stt(s2[:, 0:n], s2[:, 0:n], 1.0, psum[b_gb][:, 0:n], ALU.add, ALU.mult, [s2k, ("ps", b_gb)], [s2k])
            stt(tg[:, 0:n], tg[:, 0:n], 1.0, psum[b_ga][:, 0:n], ALU.add, ALU.mult, [tgk, ("ps", b_ga)], [tgk])
            tt(tg[:, 0:n], psum[b_ba][:, 0:n], tg[:, 0:n], ALU.mult, [("ps", b_ba), tgk], [tgk])
            for kk in range(1, 3):
                stt(cv[:, 0:np_], cu[:, kk:kk + np_], caw(kk), cv[:, 0:np_], ALU.mult, ALU.add,
                    [cuk, cvk, ("dv",)], [cvk])
                if ns_:
                    stt(cv3, cu3[:, :, kk:kk + ST], caw(kk), cv3, ALU.mult, ALU.add, [cuSk, cvk, ("dv",)], [cvk])
            if last_p:
                cp(sop[:, l, j, 0:2], cu[:, np_:np_ + 2], [cuk], [sopk], eng="pool")
            if ns_:
                cp(sos[:, l, j, :, 0:2], cu3[:, :, 4:6], [cuSk], [sosk], eng="pool")
            tt(ya[:, j, col:col + n], cv[:, 0:n], tg[:, 0:n], ALU.mult, [cvk, tgk], [("ya", j, ti)], eng="pool")
            return dict(g=g, l=l, j=j, ti=ti, tile=tile, xc=xc, xck=xck, xcb=xcb, xcbk=xcbk, s2=s2, s2k=s2k,
                        np_=np_, ns_=ns_, last_p=last_p)

        hprev = [None, None]
        hbprev = [None, None]

        def ab_back(c):
            l, j, ti = c["l"], c["j"], c["ti"]
            kind, n, col, gofs = c["tile"]
            np_, ns_, last_p = c["np_"], c["ns_"], c["last_p"]
            xc, xck, xcb, xcbk, s2, s2k = c["xc"], c["xck"], c["xcb"], c["xcbk"], c["s2"], c["s2k"]
            lj = l * KC + j
            sopk = ("sop", l, j)
            sosk = ("sos", l, j)
            b_zr = bank()
            mm(b_zr, n, [(wrbd[:, l, j, :], xcb[:, 0:n])], [xcbk, ("wrbd",)])
            b_zi = bank()
            mm(b_zi, n, [(wibd[:, l, j, :], xcb[:, 0:n])], [xcbk, ("wibd",)])
            tr, trk = wk("tr")
            ti_, tik = wk("ti")
            act(tr[:, 0:n], psum[b_zr][:, 0:n], AF.Tanh, [("ps", b_zr), ("dv",)], [trk],
                bias=dcol(DV_BRH + lj), scale=0.5)
            act(ti_[:, 0:n], psum[b_zi][:, 0:n], AF.Tanh, [("ps", b_zi), ("dv",)], [tik],
                bias=dcol(DV_BIH + lj), scale=0.5)
            a_, ak = wk("a")
            th, thk = wk("th")
            act(a_[:, 0:n], tr[:, 0:n], AF.Exp, [trk, ("dv",)], [ak], bias=dcol(DV_CH + lj), scale=dcol(DV_CH + lj))
            if True:
                act(th[:, 0:n], tr[:, 0:n], AF.Exp, [trk, ("dv",)], [thk], bias=dcol(DV_CQ + lj), scale=dcol(DV_CQ + lj))
                act(th[:, 0:n], th[:, 0:n], AF.Relu, [thk], [thk], bias=0.25, scale=-0.25)
                act(th[:, 0:n], th[:, 0:n], AF.Sqrt, [thk], [thk])
                stt(ti_[:, 0:n], ti_[:, 0:n], 1.0, xc[:, 0:n], ALU.add, ALU.mult, [tik, xck], [tik])
            else:
                act(th[:, 0:n], tr[:, 0:n], AF.Tanh, [trk, ("dv",)], [thk], bias=dcol(DV_C4 + lj), scale=dcol(DV_C4 + lj))
                act(th[:, 0:n], th[:, 0:n], AF.Sqrt, [thk], [thk], scale=-0.25)
                stt(ti_[:, 0:n], ti_[:, 0:n], 1.0, xc[:, 0:n], ALU.add, ALU.mult, [tik, xck], [tik])
                stt(th[:, 0:n], a_[:, 0:n], 1.0, th[:, 0:n], ALU.add, ALU.mult, [ak, thk], [thk])
            tt(ti_[:, 0:n], th[:, 0:n], ti_[:, 0:n], ALU.mult, [thk, tik], [tik])
            h, hk = wk("h")
            if ti == 0:
                init_ap, init_k = sop[:, l, j, 5:6], sopk
            else:
                init_ap, init_k = hprev[0], hprev[1]
            if ns_:
                a3 = v3(a_[:, np_:n], ST)
                u3 = v3(ti_[:, np_:n], ST)
                scr = h[:, np_:np_ + SB]
                tt(scr, a3[:, :, 0], sH[:, l, j, :], ALU.mult, [ak, ("sH",)], [hk])
                tt(u3[:, :, 0], u3[:, :, 0], scr, ALU.add, [tik, hk], [tik])
                P.op("dve", lambda e: e.memset(a3[:, :, 0], 0.0), [], [ak])
            P.op("dve", lambda e: e.tensor_tensor_scan(
                out=h[:, 0:n], data0=a_[:, 0:n], data1=ti_[:, 0:n], initial=init_ap,
                op0=ALU.mult, op1=ALU.add), [ak, tik, init_k], [hk])
            hprev[0], hprev[1] = h[:, np_ - 1:np_], hk
            if last_p:
                cp(sop[:, l, j, 5:6], h[:, np_ - 1:np_], [hk], [sopk], eng="pool")
            if ns_:
                h3 = v3(h[:, np_:n], ST)
                cp(sos[:, l, j, :, 5], h3[:, :, ST - 1], [hk], [sosk], eng="pool")
            tt(yb[:, j, col:col + n], h[:, 0:n], s2[:, 0:n], ALU.mult, [hk, s2k], [("yb", j, ti)], eng="pool")

        def mg_unit(ui, g, l, j, pumpf):
            slot = ui % NSLOT
            wkey = ("w", slot)
            for ti, (kind, n, col, gofs) in enumerate(tiles_of(g)):
                pumpf()
                def grp(off, src, skey):
                    bk = bank()
                    mm(bk, n, [(wview(slot, off + k * 128), src[:, k, col:col + n]) for k in range(KC)],
                       [(skey, k, ti) for k in range(KC)] + [wkey])
                    return bk
                b_ma = grp(0, hn, "hn")
                b_pa = grp(2048, ya, "ya")
                b_mb = grp(1024, hn, "hn")
                b_pb = grp(3072, yb, "yb")
                q1, q1k = wk("tg")
                q2, q2k = wk("cv")
                act(q1[:, 0:n], psum[b_ma][:, 0:n], AF.Tanh, [("ps", b_ma)], [q1k], scale=0.5)
                stt(q1[:, 0:n], q1[:, 0:n], 1.0, psum[b_pa][:, 0:n], ALU.add, ALU.mult, [q1k, ("ps", b_pa)], [q1k])
                act(q2[:, 0:n], psum[b_mb][:, 0:n], AF.Tanh, [("ps", b_mb)], [q2k], scale=0.5)
                stt(q2[:, 0:n], q2[:, 0:n], 1.0, psum[b_pb][:, 0:n], ALU.add, ALU.mult, [q2k, ("ps", b_pb)], [q2k])
                stt(mg[:, j, col:col + n], q2[:, 0:n], 0.5, q1[:, 0:n], ALU.mult, ALU.add, [q1k, q2k], [("mg", j, ti)])

        def wo_unit(ui, g, l, j, pumpf):
            slot = ui % NSLOT
            wkey = ("w", slot)
            for ti, (kind, n, col, gofs) in enumerate(tiles_of(g)):
                pumpf()
                bk = bank()
                mm(bk, n, [(wview(slot, k * 128), mg[:, k, col:col + n]) for k in range(KC)],
                   [("mg", k, ti) for k in range(KC)] + [wkey])
                stt(x[:, j, col:col + n], psum[bk][:, 0:n], 0.5, x[:, j, col:col + n], ALU.mult, ALU.add,
                    [("ps", bk), ("x", j, ti)], [("x", j, ti)])

        def gt_unit(ui, g, l, j, pumpf):
            slot = ui % NSLOT
            wkey = ("w", slot)
            for ti, (kind, n, col, gofs) in enumerate(tiles_of(g)):
                pumpf()
                b_pg = bank()
                mm(b_pg, n, [(wview(slot, k * 128), hn[:, k, col:col + n]) for k in range(KC)],
                   [("hn", k, ti) for k in range(KC)] + [wkey])
                b_pe = bank()
                mm(b_pe, n, [(wview(slot, 1024 + k * 128), pT[:, k, col:col + n]) for k in range(2)],
                   [("pT", ti), wkey])
                q1, q1k = wk("tg")
                act(q1[:, 0:n], psum[b_pg][:, 0:n], AF.Tanh, [("ps", b_pg)], [q1k], scale=0.5)
                stt(q1[:, 0:n], q1[:, 0:n], 1.0, psum[b_pe][:, 0:n], ALU.add, ALU.mult, [q1k, ("ps", b_pe)], [q1k])
                stt(x[:, j, col:col + n], q1[:, 0:n], 0.5, x[:, j, col:col + n], ALU.mult, ALU.add,
                    [q1k, ("x", j, ti)], [("x", j, ti)])

        ui = 0
        prev_stores = []
        for g in range(2):
            tl = tiles_of(g)
            for ti, (kind, n, col, gofs) in enumerate(tl):
                extra = [sv for (c0, c1, sv) in prev_stores if c0 < col + n and col < c1]
                P.dma("sp", [lambda e, col=col, n=n, gofs=gofs: e.dma_start(out=x[:, :, col:col + n],
                                                                        in_=xin[:, :, gofs:gofs + n])],
                      f"d_x{ti}", [], [("x", k, ti) for k in range(KC)], extra=extra)
            for l in range(L):
                pump(ui, 0, force_upto=ui + 1)
                if gate_fns:
                    P.dma("pool", gate_fns, "d_misc2", [], [("wrbd",), ("wibd",)])
                    gate_fns = None
                P.dma("pool", [lambda e, col=col, n=n, gofs=gofs, l=l: e.dma_start(
                    out=pT[:, :, col:col + n], in_=pin[l, :, :, gofs:gofs + n])
                    for (kind, n, col, gofs) in tl],
                      "d_p", [], [("pT", ti) for ti in range(len(tl))])
                norm_phase(g, V_NIN + l * 8, hn)
                pend = None
                for j in range(KC):
                    assert units[ui] == (g, l, "B", j) and units[ui + 1] == (g, l, "A", j)
                    pump(ui, 0, force_upto=ui + 1)
                    for ti, tile in enumerate(tl):
                        pump(ui, 99)
                        c = ab_front(ui, g, l, j, ti, tile)
                        if pend is not None:
                            ab_back(pend)
                        pend = c
                    ui += 2
                ab_back(pend)
                for kind_u, fn_u in (("MG", mg_unit), ("WO", wo_unit)):
                    for j in range(KC):
                        assert units[ui] == (g, l, kind_u, j)
                        pump(ui, 0, force_upto=ui)
                        fn_u(ui, g, l, j, lambda ui=ui: pump(ui, 99))
                        ui += 1
                norm_phase(g, V_NPE + l * 8, hn)
                for j in range(KC):
                    assert units[ui] == (g, l, "GT", j)
                    pump(ui, 0, force_upto=ui)
                    gt_unit(ui, g, l, j, lambda ui=ui: pump(ui, 99))
                    ui += 1
            stores = []

            def store_tile(ti, tl=tl, stores=stores):
                kind, n, col, gofs = tl[ti]
                P.dma("sp", [lambda e: e.dma_start(out=yT[:, :, gofs:gofs + n], in_=x[:, :, col:col + n])],
                      f"d_y{ti}", [("x", k, ti) for k in range(KC)], [])
                stores.append((col, col + n, (f"d_y{ti}", P.cnt[f"d_y{ti}"])))

            norm_phase(g, V_NF, None, final=True, after_tile=store_tile)
            prev_stores = stores
        sop_keys = [("sop", l, j) for l in range(L) for j in range(KC)]
        sos_keys = [("sos", l, j) for l in range(L) for j in range(KC)]
        P.dma("sp", [lambda e: e.dma_start(out=sop_d, in_=sop[:])], "d_so", sop_keys, [])
        P.dma("sp", [lambda e: e.dma_start(out=sos_d, in_=sos[:])], "d_so", sos_keys, [])
        P.final_wait("sp", [f"d_y{i}" for i in range(3)] + ["d_so"])

        with nc.Block() as block:
            def replay(name):
                fuse = name in ("act", "dve", "pool")

                def run(e):
                    pend = []
                    for item in P.stream[name]:
                        if item[0] == "wait":
                            if fuse:
                                pend.append(item)
                            else:
                                e.wait_ge(SEM[item[1]], item[2])
                        else:
                            if item[3] != 1:
                                for w in pend:
                                    e.wait_ge(SEM[w[1]], w[2])
                                pend = []
                            for w in pend[:-1]:
                                e.wait_ge(SEM[w[1]], w[2])
                            ins = item[1](e)
                            if pend:
                                ins._wait_ge(SEM[pend[-1][1]], pend[-1][2])
                            pend = []
                            ins.then_inc(SEM[item[2]], item[3])
                    for w in pend:
                        e.wait_ge(SEM[w[1]], w[2])
                return run

            block.tensor(replay("pe"))
            block.scalar(replay("act"))
            block.vector(replay("dve"))
            block.gpsimd(replay("pool"))
            block.sync(replay("sp"))
    return nc


def _fm(v):
    v = np.asarray(v, np.float32)
    lead = v.shape[:-1]
    r = v.reshape(lead + (KC, 128))
    return np.ascontiguousarray(np.moveaxis(r, -1, 0))


_NC_CACHE = {}


def kernel(x_prompt, x_sample, state_conv_a, state_conv_b, state_h, p_prompt, p_sample,
           norm_in, w_in, conv_a_w, conv_b_w, conv_b_b, w_r, b_r, w_i, b_i, lam,
           w_a_out, w_b_out, w_o, norm_pe, w_pg, w_pe, norm_final):
    f32 = np.float32
    vecs = np.zeros((128, NV), f32)
    vecs[:, V_NIN:V_NIN + 16] = _fm(norm_in).reshape(128, 16)
    vecs[:, V_CAW:V_CAW + 48] = _fm(conv_a_w).reshape(128, 48)
    vecs[:, V_CBW:V_CBW + 64] = _fm(conv_b_w).reshape(128, 64)
    vecs[:, V_CBB:V_CBB + 16] = _fm(conv_b_b).reshape(128, 16)
    vecs[:, V_BR:V_BR + 16] = _fm(b_r).reshape(128, 16)
    vecs[:, V_BI:V_BI + 16] = _fm(b_i).reshape(128, 16)
    vecs[:, V_LAM:V_LAM + 16] = _fm(lam).reshape(128, 16)
    vecs[:, V_NPE:V_NPE + 16] = _fm(norm_pe).reshape(128, 16)
    vecs[:, V_NF:V_NF + 8] = _fm(norm_final).reshape(128, 8)

    shared = dict(
        vecs=vecs,
        w_in=np.ascontiguousarray(w_in, f32), w_r=np.ascontiguousarray(w_r, f32),
        w_i=np.ascontiguousarray(w_i, f32), w_a_out=np.ascontiguousarray(w_a_out, f32),
        w_b_out=np.ascontiguousarray(w_b_out, f32), w_o=np.ascontiguousarray(w_o, f32),
        w_pg=np.ascontiguousarray(w_pg, f32), w_pe=np.ascontiguousarray(w_pe, f32),
    )
    in_maps = []
    for c in range(NCORES):
        ss = slice(c * SB, (c + 1) * SB)
        xp = np.asarray(x_prompt[c], f32)
        xs = np.asarray(x_sample[ss], f32).reshape(NS, D)
        toks = np.concatenate([xp[:PSPLIT], xs, xp[PSPLIT:]], axis=0)
        xin = np.ascontiguousarray(toks.reshape(NTOK, KC, 128).transpose(2, 1, 0))
        pin = np.empty((L, 128, 2, NTOK), f32)
        for l in range(L):
            pp = np.asarray(p_prompt[l, c], f32)
            ps_ = np.asarray(p_sample[l, ss], f32).reshape(NS, PLE)
            pt = np.concatenate([pp[:PSPLIT], ps_, pp[PSPLIT:]], axis=0)
            pin[l] = pt.reshape(NTOK, 2, 128).transpose(2, 1, 0)
        sca = np.asarray(state_conv_a[:, ss], f32).reshape(L, SB, 2, KC, 128).transpose(4, 0, 3, 1, 2)
        scb = np.asarray(state_conv_b[:, ss], f32).reshape(L, SB, 3, KC, 128).transpose(4, 0, 3, 1, 2)
        sh = np.asarray(state_h[:, ss], f32).reshape(L, SB, KC, 128).transpose(3, 0, 2, 1)
        m = dict(shared)
        m.update(xin=xin, pin=pin, scaT=np.ascontiguousarray(sca), scbT=np.ascontiguousarray(scb),
                 shT=np.ascontiguousarray(sh))
        in_maps.append(m)

    if "nc" not in _NC_CACHE:
        _NC_CACHE["nc"] = build_nc()
    nc = _NC_CACHE["nc"]
    res = run_bass_kernel_spmd(nc, in_maps, core_ids=list(range(NCORES)))

    y_prompt = np.empty((NCORES, SEQ, D), f32)
    y_sample = np.empty((NCORES * SB, ST, D), f32)
    ca_p = np.empty((L, NCORES, 2, D), f32)
    cb_p = np.empty((L, NCORES, 3, D), f32)
    h_p = np.empty((L, NCORES, D), f32)
    ca_s = np.empty((L, NCORES * SB, 2, D), f32)
    cb_s = np.empty((L, NCORES * SB, 3, D), f32)
    h_s = np.empty((L, NCORES * SB, D), f32)
    for c in range(NCORES):
        r = res.results[c]
        ss = slice(c * SB, (c + 1) * SB)
        yt = np.asarray(r["yT"]).transpose(2, 1, 0).reshape(NTOK, D)
        y_prompt[c, :PSPLIT] = yt[:PSPLIT]
        y_prompt[c, PSPLIT:] = yt[PSPLIT + NS:]
        y_sample[ss] = yt[PSPLIT:PSPLIT + NS].reshape(SB, ST, D)
        sop = np.asarray(r["sop"])
        so = sop.transpose(1, 3, 2, 0).reshape(L, 6, D)
        ca_p[:, c] = so[:, 0:2]
        cb_p[:, c] = so[:, 2:5]
        h_p[:, c] = so[:, 5]
        sos = np.asarray(r["sos"])
        s2 = sos.transpose(1, 3, 4, 2, 0).reshape(L, SB, 6, D)
        ca_s[:, ss] = s2[:, :, 0:2]
        cb_s[:, ss] = s2[:, :, 2:5]
        h_s[:, ss] = s2[:, :, 5]
    return (y_prompt, y_sample, ca_p, cb_p, h_p, ca_s, cb_s, h_s)
```
